# Optimizing a Trainium2 kernel written in Bass

```python
import jax, jax.numpy as jnp
from jax import lax
import numpy as np

D_MODEL = 2048
BATCH = 2
SEQ = 4096
DEPTH = 2

CHUNK = 64
N_MIXERS = 2
N_POOL_LAYERS = (DEPTH + 1) // 2
N_RWKV_LAYERS = DEPTH // 2

POOL_WINDOWS = (2, 4, 8, 16)
N_POOL_GROUPS = len(POOL_WINDOWS)
POOL_GROUP_DIM = D_MODEL // N_POOL_GROUPS

HEAD_SIZE = 64
N_HEADS = D_MODEL // HEAD_SIZE
DECAY_LORA = max(32, int(round(1.8 * D_MODEL ** 0.5 / 32)) * 32)
AAA_LORA = max(32, int(round(1.8 * D_MODEL ** 0.5 / 32)) * 32)
GATE_LORA = max(32, int(round(0.6 * D_MODEL ** 0.8 / 32)) * 32)
N_SHIFT_MIX = 6
GN_EPS = 64e-5
L2_EPS = 1e-12

D_FF = -(-8 * D_MODEL // (3 * 256)) * 256
RMS_EPS = 1e-6

kernel_name = 'hybrid_pool_rwkv7_encoder'


def rms_norm(x, g):
    xf = x.astype(jnp.float32)
    y = xf * lax.rsqrt(jnp.mean(xf * xf, axis=-1, keepdims=True) + RMS_EPS)
    return (y * g.astype(jnp.float32)).astype(x.dtype)


def swiglu_ffn(h, w1, w3, w2):
    return (jax.nn.silu(h @ w1) * (h @ w3)) @ w2


def multiscale_pool_mixer(h, w_grp, b_grp, scale):
    B, S, D = h.shape
    hf = h.astype(jnp.float32)
    cs = jnp.cumsum(hf, axis=1)
    t = jnp.arange(S)
    groups = []
    for g, win in enumerate(POOL_WINDOWS):
        csg = cs[:, :, g * POOL_GROUP_DIM:(g + 1) * POOL_GROUP_DIM]
        lagged = jnp.pad(csg, ((0, 0), (win, 0), (0, 0)))[:, :S]
        count = jnp.minimum(t + 1, win).astype(jnp.float32)[None, :, None]
        groups.append((csg - lagged) / count - hf[:, :, g * POOL_GROUP_DIM:(g + 1) * POOL_GROUP_DIM])
    pooled = jnp.stack(groups, axis=2)
    mixed = jnp.einsum('bsgc,gcd->bsgd', pooled, w_grp.astype(jnp.float32)) + b_grp.astype(jnp.float32)
    return (mixed.reshape(B, S, D) * scale.astype(jnp.float32)).astype(h.dtype)


def rwkv7_time_mix(h, mu, w_r, w_k, w_v, w_o, w0, w_la, w_lb, a0, a_la, a_lb,
                   g_la, g_lb, k_k, k_a, r_k, lnx_w, lnx_b):
    B, S, D = h.shape
    f32 = jnp.float32
    xx = jnp.pad(h, ((0, 0), (1, 0), (0, 0)))[:, :S] - h
    xr, xw, xk, xv, xa, xg = (h + xx * mu[j] for j in range(N_SHIFT_MIX))
    r = xr @ w_r
    w_log = -jax.nn.softplus(-(w0 + jnp.tanh(xw @ w_la) @ w_lb)) - 0.5
    k = xk @ w_k
    v = xv @ w_v
    a = jax.nn.sigmoid(a0 + (xa @ a_la) @ a_lb)
    g = jax.nn.sigmoid(xg @ g_la) @ g_lb

    def heads(t):
        return t.astype(f32).reshape(B, S, N_HEADS, HEAD_SIZE)

    kk = heads(k * k_k)
    kk = kk / jnp.maximum(jnp.sqrt(jnp.sum(kk * kk, axis=-1, keepdims=True)), L2_EPS)
    k = k * (1 + (a - 1) * k_a)
    rh, kh, vh, ah = heads(r), heads(k), heads(v), heads(a)
    decay = jnp.exp(-jnp.exp(heads(w_log)))
    seq_in = tuple(jnp.moveaxis(t, 1, 0) for t in (rh, decay, kh, vh, -kk, kk * ah))

    def step(state, inp):
        r_t, w_t, k_t, v_t, a_t, b_t = inp
        sa = jnp.einsum('bhvk,bhk->bhv', state, a_t)
        state = (state * w_t[:, :, None, :] + sa[..., None] * b_t[:, :, None, :]
                 + v_t[..., None] * k_t[:, :, None, :])
        return state, jnp.einsum('bhvk,bhk->bhv', state, r_t)

    state0 = jnp.zeros((B, N_HEADS, HEAD_SIZE, HEAD_SIZE), f32)
    _, y = lax.scan(step, state0, seq_in)
    y = jnp.moveaxis(y, 0, 1)
    mean = jnp.mean(y, axis=-1, keepdims=True)
    var = jnp.mean(jnp.square(y - mean), axis=-1, keepdims=True)
    y = ((y - mean) * lax.rsqrt(var + GN_EPS)).reshape(B, S, D) * lnx_w.astype(f32) + lnx_b.astype(f32)
    bonus = jnp.sum(rh * kh * r_k.astype(f32), axis=-1, keepdims=True) * vh
    y = (y + bonus.reshape(B, S, D)).astype(h.dtype)
    return (y * g) @ w_o


def setup_inputs(seed: int = 0) -> dict:
    key = jax.random.key(seed)
    ks = jax.random.split(key, 32)
    f32 = jnp.float32
    D, C, G, H, N, F = D_MODEL, POOL_GROUP_DIM, N_POOL_GROUPS, N_HEADS, HEAD_SIZE, D_FF
    NP, NR, L = N_POOL_LAYERS, N_RWKV_LAYERS, DEPTH

    def nrm(k, shape, scale):
        return jax.random.normal(k, shape, f32) * scale

    def unif(k, shape, lo, hi):
        return jax.random.uniform(k, shape, f32, lo, hi)

    return {
        'x': nrm(ks[0], (BATCH, SEQ, D), 1.0),
        'norm1_g': 1.0 + nrm(ks[1], (L, D), 0.02),
        'norm2_g': 1.0 + nrm(ks[2], (L, D), 0.02),
        'final_g': 1.0 + nrm(ks[3], (D,), 0.02),
        'pool_w': nrm(ks[4], (NP, G, C, C), C ** -0.5),
        'pool_b': nrm(ks[5], (NP, G, C), 0.02),
        'pool_scale': unif(ks[6], (NP, D), 0.1, 0.5),
        'rw_mu': unif(ks[7], (NR, N_SHIFT_MIX, D), 0.0, 1.0),
        'rw_r': nrm(ks[8], (NR, D, D), D ** -0.5),
        'rw_k': nrm(ks[9], (NR, D, D), D ** -0.5),
        'rw_v': nrm(ks[10], (NR, D, D), D ** -0.5),
        'rw_o': nrm(ks[11], (NR, D, D), D ** -0.5 * 0.5),
        'rw_w0': unif(ks[12], (NR, D), -5.0, 1.0),
        'rw_w_la': nrm(ks[13], (NR, D, DECAY_LORA), D ** -0.5),
        'rw_w_lb': nrm(ks[14], (NR, DECAY_LORA, D), DECAY_LORA ** -0.5 * 0.5),
        'rw_a0': nrm(ks[15], (NR, D), 0.5),
        'rw_a_la': nrm(ks[16], (NR, D, AAA_LORA), D ** -0.5),
        'rw_a_lb': nrm(ks[17], (NR, AAA_LORA, D), AAA_LORA ** -0.5 * 0.5),
        'rw_g_la': nrm(ks[18], (NR, D, GATE_LORA), D ** -0.5),
        'rw_g_lb': nrm(ks[19], (NR, GATE_LORA, D), GATE_LORA ** -0.5),
        'rw_k_k': 0.85 + nrm(ks[20], (NR, D), 0.05),
        'rw_k_a': 1.0 + nrm(ks[21], (NR, D), 0.05),
        'rw_r_k': nrm(ks[22], (NR, H, N), 0.1),
        'rw_lnx_w': 1.0 + nrm(ks[23], (NR, D), 0.02),
        'rw_lnx_b': nrm(ks[24], (NR, D), 0.02),
        'ffn_w1': nrm(ks[25], (L, D, F), D ** -0.5),
        'ffn_w3': nrm(ks[26], (L, D, F), D ** -0.5),
        'ffn_w2': nrm(ks[27], (L, F, D), F ** -0.5 * 0.5),
    }


def reference(x, norm1_g, norm2_g, final_g, pool_w, pool_b, pool_scale,
              rw_mu, rw_r, rw_k, rw_v, rw_o, rw_w0, rw_w_la, rw_w_lb,
              rw_a0, rw_a_la, rw_a_lb, rw_g_la, rw_g_lb, rw_k_k, rw_k_a, rw_r_k,
              rw_lnx_w, rw_lnx_b, ffn_w1, ffn_w3, ffn_w2):
    h = x
    for i in range(DEPTH):
        hn = rms_norm(h, norm1_g[i])
        j = i // N_MIXERS
        if i % N_MIXERS == 0:
            h = h + multiscale_pool_mixer(hn, pool_w[j], pool_b[j], pool_scale[j])
        else:
            h = h + rwkv7_time_mix(hn, rw_mu[j], rw_r[j], rw_k[j], rw_v[j], rw_o[j],
                                   rw_w0[j], rw_w_la[j], rw_w_lb[j],
                                   rw_a0[j], rw_a_la[j], rw_a_lb[j],
                                   rw_g_la[j], rw_g_lb[j], rw_k_k[j], rw_k_a[j], rw_r_k[j],
                                   rw_lnx_w[j], rw_lnx_b[j])
        h = h + swiglu_ffn(rms_norm(h, norm2_g[i]), ffn_w1[i], ffn_w3[i], ffn_w2[i])
    return rms_norm(h, final_g)
```

```python
import numpy as np
from contextlib import ExitStack
import concourse.bass as bass
import concourse.mybir as mybir
from concourse.bass_utils import run_bass_kernel_spmd

F32 = mybir.dt.float32
BF16 = mybir.dt.bfloat16
AF = mybir.ActivationFunctionType
ALU = mybir.AluOpType

D = 2048
NC16 = 16
FF = 5632
NFC = 44
NSPLIT = 4
FPS = NFC // NSPLIT
NT = 1024
HL = 16
SEM_LIMIT = 30000
EPS = 1e-6


class Dep:
    __slots__ = ("w", "r", "dsem", "dval")

    def __init__(self):
        self.w = None
        self.r = []
        self.dsem = None
        self.dval = 0


class Sched:
    ENGS = ("pe", "act", "dve", "pool", "sp")

    def __init__(self, nc, stack):
        self.nc = nc
        self.stack = stack
        self.sems = {}
        self.prog = {e: [] for e in self.ENGS}
        self.cur = {}
        self.gen = {e: 0 for e in self.ENGS}
        self.seen = {e: {} for e in self.ENGS}
        self.nsem = 0
        for e in self.ENGS:
            self._new_eng_sem(e)
        self.psum = []
        self.psum_i = 0

    def _alloc(self, key):
        h = self.stack.enter_context(self.nc.semaphore(f"s{self.nsem}"))
        self.nsem += 1
        self.sems[key] = h

    def _new_eng_sem(self, e):
        key = (e, self.gen[e])
        self.gen[e] += 1
        self._alloc(key)
        self.cur[e] = [key, 0]

    def _waits(self, e, deps):
        seen = self.seen[e]
        best = {}
        for d in deps:
            if d is None:
                continue
            k, v = d
            if e == "pe" and k[0] == "pe":
                continue
            if seen.get(k, 0) >= v:
                continue
            if best.get(k, 0) < v:
                best[k] = v
        for k, v in best.items():
            seen[k] = v
        return list(best.items())

    @staticmethod
    def _deps(reads, writes):
        deps = []
        for d in reads:
            deps.append(d.w)
        for d in writes:
            deps.append(d.w)
            deps.extend(d.r)
        return deps

    def op(self, e, fn, reads=(), writes=(), inc=True):
        waits = self._waits(e, self._deps(reads, writes))
        cur = self.cur[e]
        if not inc:
            tag = (cur[0], cur[1] + 1)
            self.prog[e].append((waits, fn, None))
            for d in reads:
                d.r.append(tag)
            for d in writes:
                d.w = tag
                d.r = []
            return
        cur[1] += 1
        tag = (cur[0], cur[1])
        self.prog[e].append((waits, fn, (cur[0], 1)))
        for d in reads:
            d.r.append(tag)
        for d in writes:
            d.w = tag
            d.r = []
        if cur[1] >= SEM_LIMIT:
            self._new_eng_sem(e)

    def ins(self, e, name, reads=(), writes=(), inc=True, **kw):
        self.op(e, lambda eng: getattr(eng, name)(**kw), reads, writes, inc)

    def dmak(self, e, reads=(), writes=(), owner=None, **kw):
        self.dma(e, lambda eng: eng.dma_start(**kw), reads, writes, owner)

    def dma(self, e, fn, reads=(), writes=(), owner=None):
        owner = owner or (writes[0] if writes else reads[0])
        if owner.dsem is None:
            owner.dsem = ("dma", self.nsem)
            self._alloc(owner.dsem)
        waits = self._waits(e, self._deps(reads, writes))
        owner.dval += 16
        tag = (owner.dsem, owner.dval)
        self.prog[e].append((waits, fn, (owner.dsem, 16)))
        for d in reads:
            d.r.append(tag)
        for d in writes:
            d.w = tag
            d.r = []

    def wait(self, e, deps):
        ds = []
        for d in deps:
            ds.append(d.w)
            ds.extend(d.r)
        self.prog[e].append((self._waits(e, ds), None, None))

    def init_psum(self, n=8):
        for i in range(n):
            t = self.stack.enter_context(self.nc.psum_tensor(f"psb{i}", [128, 512], F32))
            self.psum.append((t, Dep()))

    def bank(self):
        t = self.psum[self.psum_i % len(self.psum)]
        self.psum_i += 1
        return t

    def emit(self):
        sems, prog = self.sems, self.prog

        def run(e, eng):
            for waits, fn, inc in prog[e]:
                for k, v in waits:
                    eng.wait_ge(sems[k], v)
                if fn is not None:
                    i = fn(eng)
                    if inc is not None:
                        i.then_inc(sems[inc[0]], inc[1])

        with self.nc.Block() as block:
            @block.tensor
            def _(eng):
                run("pe", eng)

            @block.scalar
            def _(eng):
                run("act", eng)

            @block.vector
            def _(eng):
                run("dve", eng)

            @block.gpsimd
            def _(eng):
                run("pool", eng)

            @block.sync
            def _(eng):
                run("sp", eng)


class Ctx:
    def __init__(self):
        self.nc = bass.Bass("TRN2", target_bir_lowering=False)
        self.st = ExitStack()
        self.S = Sched(self.nc, self.st)
        self.S.init_psum()
        self.n = 0

    def sb(self, shape, dt):
        self.n += 1
        return self.st.enter_context(self.nc.sbuf_tensor(f"t{self.n}", list(shape), dt))

    def dram_in(self, name, shape, dt=F32):
        return self.nc.dram_tensor(name, list(shape), dt, kind="ExternalInput").ap()

    def dram_out(self, name, shape, dt=F32):
        return self.nc.dram_tensor(name, list(shape), dt, kind="ExternalOutput").ap()

    def finish(self):
        self.S.emit()
        self.st.close()
        return self.nc


def alias_barrier(old, new):
    tags = []
    for d in old:
        if d.w is not None:
            tags.append(d.w)
        tags.extend(d.r)
    for d in new:
        d.r.extend(tags)


def load_consts(C):
    S = C.S
    ones = C.sb([128, 128], BF16)
    d = Dep()
    S.op("pool", lambda e: e.memset(ones[:], 1.0), writes=[d])
    C.ones, C.d_ones = ones, d


def make_eps(C, val=EPS):
    t = C.sb([128, 1], F32)
    d = Dep()
    C.S.op("pool", lambda e: e.memset(t[:], val), writes=[d])
    C.epsb, C.d_epsb = t, d


def load_vec(C, dram_ap, ncols):
    t = C.sb([128, ncols], F32)
    d = Dep()
    C.S.dma("sp", lambda e: e.dma_start(out=t[:], in_=dram_ap), writes=[d])
    return t, d


def rms_rstd(C, src, dsrc, cols, rstd, d_rstd, sq, d_sq):
    S = C.S
    for (o, n) in cols:
        ps, dps = S.bank()
        for c in range(NC16):
            b = c % 2
            S.ins("act", "activation", [dsrc[c]], [d_sq[b]], out=sq[:, b, 0:n], in_=src(c, o, n), func=AF.Square)
            S.ins("pe", "matmul", [C.d_ones, d_sq[b]], [dps], out=ps[:, 0:n], lhsT=C.ones[:], rhs=sq[:, b, 0:n],
                  start=(c == 0), stop=(c == NC16 - 1))
        S.ins("act", "activation", [dps, C.d_epsb], [d_rstd], out=rstd[:, o:o + n], in_=ps[:, 0:n], func=AF.Sqrt,
              bias=C.epsb[:, 0:1], scale=1.0 / D)
        S.ins("dve", "reciprocal", [d_rstd], [d_rstd], out=rstd[:, o:o + n], in_=rstd[:, o:o + n])


def ffn_block(C, h, dh, hoff, xn, dxn, w13_d, w2_d, gT, dgT, w13t, dw13, w2t, dw2, tmp, dtmp):
    S = C.S
    NW13 = len(dw13)
    NW2 = len(dw2)
    i13 = 0
    i2 = 0
    it = 0
    for s in range(NSPLIT):
        gb = s % 2
        for fi in range(FPS):
            fo = s * FPS + fi
            wb = i13 % NW13
            i13 += 1
            S.dma("pool", lambda e, fo=fo, wb=wb: e.dma_start(out=w13t[:, wb], in_=w13_d[fo]),
                  writes=[dw13[wb]])
            banks = [S.bank() for _ in range(4)]
            for kc in range(NC16):
                for m in range(2):
                    for tt in range(2):
                        ps, dps = banks[m * 2 + tt]
                        S.op("pe", lambda e, ps=ps, wb=wb, m=m, kc=kc, tt=tt: e.matmul(
                            ps[:], w13t[:, wb, m, kc, :], xn[:, kc, tt * 512:(tt + 1) * 512],
                            start=(kc == 0), stop=(kc == NC16 - 1)),
                            reads=[dw13[wb], dxn[kc]], writes=[dps], inc=(kc == NC16 - 1))
            for tt in range(2):
                p1, d1 = banks[tt]
                p3, d3 = banks[2 + tt]
                tb = it % 2
                it += 1
                S.op("act", lambda e, p1=p1, tb=tb: e.activation(out=tmp[:, tb], in_=p1[:], func=AF.Silu),
                     reads=[d1], writes=[dtmp[tb]])
                S.op("dve", lambda e, p3=p3, tb=tb, gb=gb, fi=fi, tt=tt: e.tensor_tensor(
                    out=gT[:, gb, fi, tt * 512:(tt + 1) * 512], in0=tmp[:, tb], in1=p3[:], op=ALU.mult),
                    reads=[dtmp[tb], d3], writes=[dgT[gb][fi]])
        for do in range(NC16):
            wb = i2 % NW2
            i2 += 1
            S.dma("pool", lambda e, s=s, do=do, wb=wb: e.dma_start(out=w2t[:, wb], in_=w2_d[s, do]),
                  writes=[dw2[wb]])
            banks = [S.bank() for _ in range(2)]
            for fi in range(FPS):
                for tt in range(2):
                    ps, dps = banks[tt]
                    S.op("pe", lambda e, ps=ps, wb=wb, fi=fi, tt=tt, gb=gb: e.matmul(
                        ps[:], w2t[:, wb, fi, :], gT[:, gb, fi, tt * 512:(tt + 1) * 512],
                        start=(fi == 0), stop=(fi == FPS - 1)),
                        reads=[dw2[wb], dgT[gb][fi]], writes=[dps], inc=(fi == FPS - 1))
            for tt in range(2):
                ps, dps = banks[tt]
                S.op("dve", lambda e, ps=ps, do=do, tt=tt: e.tensor_tensor(
                    out=h[:, do, hoff + tt * 512:hoff + (tt + 1) * 512],
                    in0=h[:, do, hoff + tt * 512:hoff + (tt + 1) * 512], in1=ps[:], op=ALU.add),
                    reads=[dps], writes=[dh[do]])


def norm_apply(C, h, dh, hoff, n, gvec, dg, gcol0, rstd, d_rstd, xn, dxn):
    S = C.S
    for c in range(NC16):
        S.op("dve", lambda e, c=c: e.scalar_tensor_tensor(
            out=xn[:, c, 0:n], in0=h[:, c, hoff:hoff + n], scalar=gvec[:, gcol0 + c:gcol0 + c + 1],
            in1=rstd[:, 0:n], op0=ALU.mult, op1=ALU.mult),
            reads=[dh[c], dg, d_rstd], writes=[dxn[c]])


def build_a1(stage=9):
    C = Ctx()
    S = C.S
    W = HL + NT
    xT = C.dram_in("xT", [D, W])
    vecs = C.dram_in("vecs", [128, 4 * NC16])
    icnt = C.dram_in("icnt", [128, 4, HL])
    pw_d = C.dram_in("pw", [4, 128, 4, 512])
    w13_d = C.dram_in("w13", [NFC, 128, 2, NC16, 128])
    w2_d = C.dram_in("w2", [NSPLIT, NC16, 128, FPS, 128])
    hT = C.dram_out("hT", [D, NT])

    load_consts(C)
    h = C.sb([128, NC16, W], F32)
    dh = [Dep() for _ in range(NC16)]
    xv = xT.rearrange("(c p) t -> p c t", p=128)
    for c in range(NC16):
        S.dma("sp", lambda e, c=c: e.dma_start(out=h[:, c, :], in_=xv[:, c, :]), writes=[dh[c]])
    vt, dv = load_vec(C, vecs, 4 * NC16)
    make_eps(C)
    bsc = C.sb([128, NC16], F32)
    dbsc = Dep()
    S.op("dve", lambda e: e.tensor_tensor(out=bsc[:], in0=vt[:, 2 * NC16:3 * NC16], in1=vt[:, 3 * NC16:4 * NC16], op=ALU.mult),
         reads=[dv], writes=[dbsc])
    ict = C.sb([128, 4, HL], F32)
    dict_ = Dep()
    S.dma("sp", lambda e: e.dma_start(out=ict[:], in_=icnt), writes=[dict_])
    pw = C.sb([128, 4, 4, 512], BF16)
    dpw = [Dep() for _ in range(4)]
    for g in range(4):
        S.dma("pool", lambda e, g=g: e.dma_start(out=pw[:, g], in_=pw_d[g]), writes=[dpw[g]])

    rstd = C.sb([128, W], F32)
    d_rstd = Dep()
    sq = C.sb([128, 2, 512], BF16)
    d_sq = [Dep(), Dep()]
    if stage >= 0.5:
        rms_rstd(C, lambda c, o, n: h[:, c, o:o + n], dh, [(0, HL), (HL, 512), (HL + 512, 512)], rstd, d_rstd, sq, d_sq)

    xn = C.sb([128, NC16, NT], BF16)
    dxn = [Dep() for _ in range(NC16)]
    scratch = C.sb([128, 2 * FPS * NT // 2], F32)
    hn = scratch[:, 0:2 * W].rearrange("p (b w) -> p b w", b=2)
    dhn = [Dep(), Dep()]
    sa = scratch[:, 2 * W:4 * W].rearrange("p (b w) -> p b w", b=2)
    dsa = [Dep(), Dep()]
    sbb = scratch[:, 4 * W:6 * W].rearrange("p (b w) -> p b w", b=2)
    dsb = [Dep(), Dep()]
    t16 = C.sb([128, 2, HL], F32)
    dt16 = [Dep(), Dep()]
    S.ins("pool", "memset", [], dsa + dsb, ap=scratch[:, 2 * W:6 * W], constant=0.0)
    for c in range(NC16 if stage >= 0.7 else 0):
        g = c // 4
        win = 2 << g
        b = c % 2
        S.op("dve", lambda e, c=c, b=b: e.scalar_tensor_tensor(
            out=hn[:, b, :], in0=h[:, c, :], scalar=vt[:, c:c + 1], in1=rstd[:], op0=ALU.mult, op1=ALU.mult),
            reads=[dh[c], dv, d_rstd], writes=[dhn[b]])
        src, dsrc = hn[:, b, :], dhn[b]
        bufs = [(sa[:, b, :], dsa[b]), (sbb[:, b, :], dsb[b])]
        for j in range(g + 1):
            sh = 1 << j
            dst, ddst = bufs[j % 2]
            S.op("pool", lambda e, dst=dst, src=src, sh=sh: e.tensor_tensor(
                out=dst[:, sh:W], in0=src[:, sh:W], in1=src[:, 0:W - sh], op=ALU.add),
                reads=[dsrc], writes=[ddst])
            src, dsrc = dst, ddst
        S.op("dve", lambda e, c=c, b=b, src=src, win=win: e.scalar_tensor_tensor(
            out=xn[:, c, :], in0=src[:, HL:W], scalar=1.0 / win, in1=hn[:, b, HL:W], op0=ALU.mult, op1=ALU.subtract),
            reads=[dsrc, dhn[b]], writes=[dxn[c]])
        S.op("dve", lambda e, b=b, src=src, g=g: e.tensor_tensor(
            out=t16[:, b, :], in0=src[:, HL:2 * HL], in1=ict[:, g, :], op=ALU.mult),
            reads=[dsrc, dict_], writes=[dt16[b]])
        S.op("dve", lambda e, c=c, b=b: e.tensor_tensor(
            out=xn[:, c, 0:HL], in0=t16[:, b, :], in1=hn[:, b, HL:2 * HL], op=ALU.subtract),
            reads=[dt16[b], dhn[b]], writes=[dxn[c]])

    tmp = C.sb([128, 2, 512], F32)
    dtmp = [Dep(), Dep()]
    it = 0
    for g in range(4 if stage >= 0.8 else 0):
        for fo in range(4):
            c = 4 * g + fo
            for tt in range(2):
                ps, dps = S.bank()
                for kc in range(4):
                    S.op("pe", lambda e, ps=ps, g=g, kc=kc, fo=fo, tt=tt: e.matmul(
                        ps[:], pw[:, g, kc, fo * 128:(fo + 1) * 128], xn[:, 4 * g + kc, tt * 512:(tt + 1) * 512],
                        start=(kc == 0), stop=(kc == 3)),
                        reads=[dpw[g], dxn[4 * g + kc]], writes=[dps], inc=(kc == 3))
                tb = it % 2
                it += 1
                if stage < 0.9:
                    continue
                S.op("act", lambda e, ps=ps, tb=tb, c=c: e.activation(
                    out=tmp[:, tb], in_=ps[:], func=AF.Identity,
                    bias=bsc[:, c:c + 1], scale=vt[:, 2 * NC16 + c:2 * NC16 + c + 1]),
                    reads=[dps, dv, dbsc], writes=[dtmp[tb]])
                if stage < 1:
                    continue
                S.op("dve", lambda e, tb=tb, c=c, tt=tt: e.tensor_tensor(
                    out=h[:, c, HL + tt * 512:HL + (tt + 1) * 512], in0=h[:, c, HL + tt * 512:HL + (tt + 1) * 512],
                    in1=tmp[:, tb], op=ALU.add),
                    reads=[dtmp[tb]], writes=[dh[c]])

    if stage >= 2:
      rms_rstd(C, lambda c, o, n: h[:, c, HL + o:HL + o + n], dh, [(0, 512), (512, 512)], rstd, d_rstd, sq, d_sq)
    if stage >= 2:
      norm_apply(C, h, dh, HL, NT, vt, dv, NC16, rstd, d_rstd, xn, dxn)
    gT = scratch.bitcast(BF16).reshape([128, 2, FPS, NT])
    dgT = [[Dep() for _ in range(FPS)] for _ in range(2)]
    w13t = C.sb([128, 3, 2, NC16, 128], BF16)
    dw13 = [Dep() for _ in range(3)]
    w2t = C.sb([128, 3, FPS, 128], BF16)
    dw2 = [Dep() for _ in range(3)]
    if stage >= 3:
      ffn_block(C, h, dh, HL, xn, dxn, w13_d, w2_d, gT, dgT, w13t, dw13, w2t, dw2, tmp, dtmp)

    hv = hT.rearrange("(c p) t -> p c t", p=128)
    dout = Dep()
    for c in range(NC16):
        S.dma("sp", lambda e, c=c: e.dma_start(out=hv[:, c, :], in_=h[:, c, HL:W]), reads=[dh[c]], owner=dout)
    S.wait("sp", [dout] + dh)
    return C.finish()


def pvec(v):
    return np.ascontiguousarray(np.asarray(v, np.float32).reshape(NC16, 128).T)


def lay_w13(w1, w3):
    a = np.stack([w1, w3], 0).reshape(2, NC16, 128, NFC, 128)
    return np.ascontiguousarray(a.transpose(3, 2, 0, 1, 4))


def lay_w2(w2):
    a = w2.reshape(NSPLIT, FPS, 128, NC16, 128)
    return np.ascontiguousarray(a.transpose(0, 3, 2, 1, 4))


def core_slices(B, Sq):
    per = NT
    out = []
    for b in range(B):
        for s0 in range(0, Sq, per):
            out.append((b, s0))
    return out


def run_a1(inp, stage=9):
    x = np.asarray(inp["x"], np.float32)
    B, Sq, _ = x.shape
    nc = build_a1(stage)
    vecs = np.concatenate([pvec(inp["norm1_g"][0]), pvec(inp["norm2_g"][0]), pvec(inp["pool_scale"][0]),
                           pvec(np.asarray(inp["pool_b"][0]).reshape(-1))], 1)
    pw = np.ascontiguousarray(np.asarray(inp["pool_w"][0], np.float32).reshape(4, 4, 128, 512).transpose(0, 2, 1, 3))
    w13 = lay_w13(np.asarray(inp["ffn_w1"][0]), np.asarray(inp["ffn_w3"][0]))
    w2 = lay_w2(np.asarray(inp["ffn_w2"][0]))
    maps = []
    for (b, s0) in core_slices(B, Sq):
        xt = np.zeros((D, HL + NT), np.float32)
        lo = max(0, s0 - HL)
        xt[:, HL - (s0 - lo):] = x[b, lo:s0 + NT].T
        ic = np.zeros((128, 4, HL), np.float32)
        for g in range(4):
            win = 2 << g
            t = np.arange(HL) + s0
            ic[:, g, :] = 1.0 / np.minimum(t + 1, win)
        maps.append({"xT": xt, "vecs": vecs, "icnt": ic, "pw": pw, "w13": w13, "w2": w2})
    res = run_bass_kernel_spmd(nc, maps, core_ids=list(range(8)))
    h1 = np.zeros((B, Sq, D), np.float32)
    for i, (b, s0) in enumerate(core_slices(B, Sq)):
        h1[b, s0:s0 + NT] = res.results[i]["hT"].T
    return h1


CH = 64
NCHK = NT // CH
NEG_EHALF = -0.6065306597126334


def build_a2():
    C = Ctx()
    S = C.S
    W1 = NT + 1
    hT = C.dram_in("hT", [D, W1])
    vecs = C.dram_in("vecs", [128, 12 * NC16])
    cmask_d = C.dram_in("cmask", [128, NT])
    wbig = {n: C.dram_in(n, [NC16, 128, NC16, 128]) for n in ("wr", "wk", "wv")}
    wla_d = C.dram_in("wla", [128, NC16, 96])
    ala_d = C.dram_in("ala", [128, NC16, 96])
    gla_d = C.dram_in("gla", [128, NC16, 256])
    wlb_d = C.dram_in("wlb", [96, D])
    alb_d = C.dram_in("alb", [96, D])
    glb_d = C.dram_in("glb", [128, 2, D])
    raw = C.dram_out("raw", [5, D, NT])
    outs = {n: C.dram_out(n, [D, NT]) for n in ("At", "Rt", "Bh", "Kh", "gg", "bonus")}
    gC_d = C.dram_out("gC", [D, NCHK])
    rawv = [raw[j].rearrange("(c p) t -> p c t", p=128) for j in range(5)]
    outv = {n: a.rearrange("(c p) t -> p c t", p=128) for n, a in outs.items()}
    gCv = gC_d.rearrange("(c p) t -> p c t", p=128)
    hv = hT.rearrange("(c p) t -> p c t", p=128)

    load_consts(C)
    make_eps(C)
    vt, dv = load_vec(C, vecs, 12 * NC16)
    VG1, VMU, VW0, VA0, VKK, VKA, VRK = 0, NC16, 7 * NC16, 8 * NC16, 9 * NC16, 10 * NC16, 11 * NC16
    cmask = C.sb([128, NT], F32)
    dcm = Dep()
    S.dmak("sp", [], [dcm], out=cmask[:], in_=cmask_d)
    blk = C.sb([128, 128], BF16)
    bq = C.sb([128, 2, NT], BF16)
    dbq = [Dep(), Dep()]
    dblk = Dep()
    S.ins("pool", "memset", [], [dblk], ap=blk[:], constant=0.0)
    S.ins("pool", "memset", [], [dblk], ap=blk[0:64, 0:64], constant=1.0)
    S.ins("pool", "memset", [], [dblk], ap=blk[64:128, 64:128], constant=1.0)
    wla = C.sb([128, NC16, 96], BF16)
    ala = C.sb([128, NC16, 96], BF16)
    gla = C.sb([128, NC16, 256], BF16)
    wlb = C.sb([96, D], BF16)
    alb = C.sb([96, D], BF16)
    glb = C.sb([128, 2, D], BF16)
    dsw = Dep()
    for t, d_ in ((wla, wla_d), (ala, ala_d), (gla, gla_d), (wlb, wlb_d), (alb, alb_d), (glb, glb_d)):
        S.dmak("pool", [], [dsw], out=t[:], in_=d_)

    scr = C.sb([128, 32768], F32)
    scr_bf = scr.bitcast(BF16)
    hn = scr_bf[:, 0:16384].rearrange("p (c t) -> p c t", c=NC16)
    dhn = [Dep() for _ in range(NC16)]
    xx = scr_bf[:, 16384:32768].rearrange("p (c t) -> p c t", c=NC16)
    dxx = [Dep() for _ in range(NC16)]
    htmp = scr[:, 16384:16384 + 3 * W1].rearrange("p (b w) -> p b w", b=3)
    dht = [Dep() for _ in range(3)]
    hnf = scr[:, 20000:20000 + 2 * W1].rearrange("p (b w) -> p b w", b=2)
    dhnf = [Dep(), Dep()]
    rstd = C.sb([128, W1], F32)
    d_rstd = Dep()
    sq = C.sb([128, 2, 512], BF16)
    d_sq = [Dep(), Dep()]
    cols = [(0, 1), (1, 512), (513, 512)]
    banks = [S.bank() for _ in cols]
    for c in range(NC16):
        tb = c % 3
        S.dmak("sp", [], [dht[tb]], out=htmp[:, tb, :], in_=hv[:, c, :])
        for ci, (o, n) in enumerate(cols):
            ps, dps = banks[ci]
            b = (c * 3 + ci) % 2
            S.ins("act", "activation", [dht[tb]], [d_sq[b]], out=sq[:, b, 0:n], in_=htmp[:, tb, o:o + n], func=AF.Square)
            S.ins("pe", "matmul", [C.d_ones, d_sq[b]], [dps], out=ps[:, 0:n], lhsT=C.ones[:], rhs=sq[:, b, 0:n],
                  start=(c == 0), stop=(c == NC16 - 1))
    for ci, (o, n) in enumerate(cols):
        ps, dps = banks[ci]
        S.ins("act", "activation", [dps, C.d_epsb], [d_rstd], out=rstd[:, o:o + n], in_=ps[:, 0:n], func=AF.Sqrt,
              bias=C.epsb[:, 0:1], scale=1.0 / D)
    S.ins("dve", "reciprocal", [d_rstd], [d_rstd], out=rstd[:], in_=rstd[:])
    for c in range(NC16):
        tb = c % 3
        fb = c % 2
        S.dmak("sp", [], [dht[tb]], out=htmp[:, tb, :], in_=hv[:, c, :])
        S.ins("dve", "scalar_tensor_tensor", [dht[tb], dv, d_rstd], [dhnf[fb]], out=hnf[:, fb, :], in0=htmp[:, tb, :],
              scalar=vt[:, VG1 + c:VG1 + c + 1], in1=rstd[:], op0=ALU.mult, op1=ALU.mult)
        S.ins("act", "activation", [dhnf[fb]], [dhn[c]], out=hn[:, c, :], in_=hnf[:, fb, 1:W1], func=AF.Copy)
        S.ins("dve", "tensor_tensor", [dhnf[fb]], [dxx[c]], out=xx[:, c, :], in0=hnf[:, fb, 0:NT], in1=hnf[:, fb, 1:W1],
              op=ALU.subtract)

    xb = scr_bf[:, 32768:65536].rearrange("p (b c t) -> p b c t", b=2, c=NC16)
    dxb = [[Dep() for _ in range(NC16)] for _ in range(2)]
    alias_barrier(dht + dhnf, [x for l in dxb for x in l])
    wt = C.sb([128, 3, NC16, 128], BF16)
    dwt = [Dep() for _ in range(3)]
    stg = C.sb([128, 4, 512], F32)
    dstg = [Dep() for _ in range(4)]
    lo = C.sb([128, 2, NT], BF16)
    dlo = [Dep(), Dep()]
    cnt = {"x": 0, "w": 0, "s": 0}

    def mix(j):
        b = cnt["x"] % 2
        cnt["x"] += 1
        for c in range(NC16):
            S.ins("dve", "scalar_tensor_tensor", [dxx[c], dhn[c], dv], [dxb[b][c]], out=xb[:, b, c, :], in0=xx[:, c, :],
                  scalar=vt[:, VMU + j * NC16 + c:VMU + j * NC16 + c + 1], in1=hn[:, c, :], op0=ALU.mult, op1=ALU.add)
        return b

    def evac_out(ps, dps, dst_ap, func=AF.Copy, rows=128):
        sb_ = cnt["s"] % 4
        cnt["s"] += 1
        S.ins("act", "activation", [dps], [dstg[sb_]], out=stg[0:rows, sb_, :], in_=ps[0:rows, :], func=func)
        S.dmak("sp", [dstg[sb_]], [], out=dst_ap, in_=stg[0:rows, sb_, :])

    douts = Dep()

    def big_proj(j, name, dst):
        b = mix(j)
        for oc in range(NC16):
            wb = cnt["w"] % 3
            cnt["w"] += 1
            S.dmak("pool", [], [dwt[wb]], out=wt[:, wb], in_=wbig[name][oc])
            bk = [S.bank() for _ in range(2)]
            for kc in range(NC16):
                for tt in range(2):
                    ps, dps = bk[tt]
                    S.ins("pe", "matmul", [dwt[wb], dxb[b][kc]], [dps], inc=(kc == NC16 - 1), out=ps[:],
                          lhsT=wt[:, wb, kc, :], rhs=xb[:, b, kc, tt * 512:(tt + 1) * 512], start=(kc == 0), stop=(kc == NC16 - 1))
            for tt in range(2):
                ps, dps = bk[tt]
                sb_ = cnt["s"] % 4
                cnt["s"] += 1
                S.ins("act", "activation", [dps], [dstg[sb_]], out=stg[:, sb_, :], in_=ps[:], func=AF.Copy)
                S.dmak("sp", [dstg[sb_]], [], owner=dstg[sb_], out=rawv[dst][:, oc, tt * 512:(tt + 1) * 512], in_=stg[:, sb_, :])

    drawall = Dep()
    dgg = Dep()

    def lora(j, A, nA, B, nkB, func1, dst, dst_is_out):
        b = mix(j)
        nch = (nA + 127) // 128
        for a_ in range(nch):
            rows = min(128, nA - a_ * 128)
            for tt in range(2):
                ps, dps = S.bank()
                for kc in range(NC16):
                    S.ins("pe", "matmul", [dsw, dxb[b][kc]], [dps], inc=(kc == NC16 - 1), out=ps[0:rows, :],
                          lhsT=A[:, kc, a_ * 128:a_ * 128 + rows], rhs=xb[:, b, kc, tt * 512:(tt + 1) * 512],
                          start=(kc == 0), stop=(kc == NC16 - 1))
                S.ins("act", "activation", [dps], [dlo[a_]], out=lo[0:rows, a_, tt * 512:(tt + 1) * 512], in_=ps[0:rows, :], func=func1)
        rowsB = min(128, nA)
        for oc in range(NC16):
            for tt in range(2):
                ps, dps = S.bank()
                for a_ in range(nch):
                    lhsT = B[0:rowsB, oc * 128:(oc + 1) * 128] if nkB == 1 else B[:, a_, oc * 128:(oc + 1) * 128]
                    S.ins("pe", "matmul", [dsw, dlo[a_]], [dps], inc=(a_ == nch - 1), out=ps[:], lhsT=lhsT,
                          rhs=lo[0:rowsB, a_, tt * 512:(tt + 1) * 512], start=(a_ == 0), stop=(a_ == nch - 1))
                sb_ = cnt["s"] % 4
                cnt["s"] += 1
                S.ins("act", "activation", [dps], [dstg[sb_]], out=stg[:, sb_, :], in_=ps[:], func=AF.Copy)
                if dst_is_out:
                    S.dmak("sp", [dstg[sb_]], [], owner=dstg[sb_], out=outv["gg"][:, oc, tt * 512:(tt + 1) * 512], in_=stg[:, sb_, :])
                else:
                    S.dmak("sp", [dstg[sb_]], [], owner=dstg[sb_], out=rawv[dst][:, oc, tt * 512:(tt + 1) * 512], in_=stg[:, sb_, :])

    lora(1, wla, 96, wlb, 1, AF.Tanh, 3, False)
    lora(4, ala, 96, alb, 1, AF.Copy, 4, False)
    lora(5, gla, 256, glb, 2, AF.Sigmoid, None, True)
    big_proj(0, "wr", 0)
    big_proj(2, "wk", 1)
    big_proj(3, "wv", 2)

    NB = 2
    e_in = scr[:, 0:NB * 5 * NT].rearrange("p (b i t) -> p b i t", b=NB, i=5)
    de_in = [[Dep() for _ in range(5)] for _ in range(NB)]
    tA = scr[:, NB * 5 * NT:NB * 11 * NT].rearrange("p (b i t) -> p b i t", b=NB, i=6)
    dtA = [[Dep() for _ in range(6)] for _ in range(NB)]
    alias_barrier(dhn + dxx + [x for l in dxb for x in l] + dstg, [x for l in de_in for x in l] + [x for l in dtA for x in l])
    gct = C.sb([128, NB, NCHK], F32)
    dgct = [Dep() for _ in range(NB)]
    for oc in range(NC16):
        b = oc % NB
        ein = lambda i, b=b: e_in[:, b, i, :]
        tt_ = lambda i, b=b: tA[:, b, i, :]
        for i in range(5):
            S.dmak("sp", [], [de_in[b][i]], out=e_in[:, b, i, :], in_=rawv[i][:, oc, :])
        R, K_, V_, ZW, ZA = range(5)
        d = de_in[b]
        t = dtA[b]
        sc = lambda base: vt[:, base + oc:base + oc + 1]
        S.ins("act", "activation", [d[ZA], dv], [d[ZA]], out=ein(ZA), in_=ein(ZA), func=AF.Sigmoid, bias=sc(VA0), scale=1.0)
        S.ins("act", "activation", [d[ZW], dv], [d[ZW]], out=ein(ZW), in_=ein(ZW), func=AF.Sigmoid, bias=sc(VW0), scale=1.0)
        S.ins("dve", "tensor_scalar", [d[ZW]], [d[ZW]], out=ein(ZW), in0=ein(ZW), scalar1=NEG_EHALF, scalar2=None, op0=ALU.mult)
        S.ins("dve", "tensor_scalar", [d[K_], dv], [t[0]], out=tt_(0), in0=ein(K_), scalar1=sc(VKK), scalar2=None, op0=ALU.mult)
        S.ins("act", "activation", [t[0]], [dbq[b]], out=bq[:, b, :], in_=tt_(0), func=AF.Square)
        for h2 in range(2):
            ps, dps = S.bank()
            S.ins("pe", "matmul", [dblk, dbq[b]], [dps], out=ps[:], lhsT=blk[:], rhs=bq[:, b, h2 * 512:(h2 + 1) * 512], start=True, stop=True)
            S.ins("act", "activation", [dps], [t[2]], out=tA[:, b, 2, h2 * 512:(h2 + 1) * 512], in_=ps[:], func=AF.Sqrt)
        S.ins("dve", "tensor_scalar", [t[2]], [t[2]], out=tt_(2), in0=tt_(2), scalar1=1e-12, scalar2=None, op0=ALU.max)
        S.ins("dve", "reciprocal", [t[2]], [t[2]], out=tt_(2), in_=tt_(2))
        S.ins("dve", "tensor_tensor", [t[0], t[2]], [t[0]], out=tt_(0), in0=tt_(0), in1=tt_(2), op=ALU.mult)
        S.ins("dve", "tensor_scalar", [d[ZA], dv], [t[1]], out=tt_(1), in0=ein(ZA), scalar1=-1.0, scalar2=sc(VKA), op0=ALU.add, op1=ALU.mult)
        S.ins("dve", "scalar_tensor_tensor", [t[1], d[K_]], [d[K_]], out=ein(K_), in0=tt_(1), scalar=1.0, in1=ein(K_), op0=ALU.add, op1=ALU.mult)
        S.ins("dve", "tensor_tensor", [t[0], d[ZA]], [d[ZA]], out=ein(ZA), in0=ein(ZA), in1=tt_(0), op=ALU.mult)
        S.ins("dve", "tensor_tensor_scan", [dcm, d[ZW]], [t[1]], out=tt_(1), data0=cmask[:], data1=ein(ZW), initial=0.0, op0=ALU.mult, op1=ALU.add)
        S.ins("dve", "tensor_tensor", [t[1], d[ZW]], [t[2]], out=tt_(2), in0=tt_(1), in1=ein(ZW), op=ALU.subtract)
        S.ins("act", "activation", [t[2]], [t[2]], out=tt_(2), in_=tt_(2), func=AF.Exp)
        S.ins("act", "activation", [t[1]], [t[3]], out=tt_(3), in_=tt_(1), func=AF.Exp)
        S.ins("act", "activation", [t[1]], [t[4]], out=tt_(4), in_=tt_(1), func=AF.Exp, scale=-1.0)
        S.ins("dve", "tensor_copy", [t[3]], [dgct[b]], out=gct[:, b, :], in_=tA[:, b, 3, :].rearrange("p (c t) -> p c t", t=CH)[:, :, CH - 1])
        S.dmak("sp", [dgct[b]], [], owner=dgct[b], out=gCv[:, oc, :], in_=gct[:, b, :])
        S.ins("dve", "scalar_tensor_tensor", [t[2], t[0]], [t[2]], out=tt_(2), in0=tt_(2), scalar=-1.0, in1=tt_(0), op0=ALU.mult, op1=ALU.mult)
        S.dmak("sp", [t[2]], [], owner=t[2], out=outv["At"][:, oc, :], in_=tt_(2))
        S.ins("dve", "scalar_tensor_tensor", [d[R], d[K_], dv], [dbq[b]], out=bq[:, b, :], in0=ein(R), scalar=sc(VRK), in1=ein(K_), op0=ALU.mult, op1=ALU.mult)
        for h2 in range(2):
            ps, dps = S.bank()
            S.ins("pe", "matmul", [dblk, dbq[b]], [dps], out=ps[:], lhsT=blk[:], rhs=bq[:, b, h2 * 512:(h2 + 1) * 512], start=True, stop=True)
            S.ins("dve", "tensor_tensor", [dps, d[V_]], [t[1]], out=tA[:, b, 1, h2 * 512:(h2 + 1) * 512], in0=e_in[:, b, V_, h2 * 512:(h2 + 1) * 512], in1=ps[:], op=ALU.mult)
        S.dmak("sp", [t[1]], [], owner=t[1], out=outv["bonus"][:, oc, :], in_=tt_(1))
        S.ins("dve", "tensor_tensor", [t[3], d[R]], [t[3]], out=tt_(3), in0=tt_(3), in1=ein(R), op=ALU.mult)
        S.dmak("sp", [t[3]], [], owner=t[3], out=outv["Rt"][:, oc, :], in_=tt_(3))
        S.ins("dve", "tensor_tensor", [t[4], d[ZA]], [d[ZA]], out=ein(ZA), in0=ein(ZA), in1=tt_(4), op=ALU.mult)
        S.dmak("sp", [d[ZA]], [], owner=d[ZA], out=outv["Bh"][:, oc, :], in_=ein(ZA))
        S.ins("dve", "tensor_tensor", [t[4], d[K_]], [t[4]], out=tt_(4), in0=tt_(4), in1=ein(K_), op=ALU.mult)
        S.dmak("sp", [t[4]], [], owner=t[4], out=outv["Kh"][:, oc, :], in_=tt_(4))
    S.wait("sp", dstg + [x for l in dtA for x in l] + [x for l in de_in for x in l] + dgct)
    return C.finish()


def lay_big(w):
    a = np.asarray(w, np.float32).reshape(NC16, 128, NC16, 128)
    return np.ascontiguousarray(a.transpose(2, 1, 0, 3))


def lay_a(w):
    n = w.shape[1]
    return np.ascontiguousarray(np.asarray(w, np.float32).reshape(NC16, 128, n).transpose(1, 0, 2))


def run_a2(inp, h1):
    B, Sq, _ = h1.shape
    nc = build_a2()
    mu = inp["rw_mu"][0]
    vecs = np.concatenate([pvec(inp["norm1_g"][1])] + [pvec(mu[j]) for j in range(6)] +
                          [pvec(inp["rw_w0"][0]), pvec(inp["rw_a0"][0]), pvec(inp["rw_k_k"][0]), pvec(inp["rw_k_a"][0]),
                           pvec(np.asarray(inp["rw_r_k"][0]).reshape(-1))], 1)
    cm = np.ones((128, NT), np.float32)
    cm[:, ::CH] = 0.0
    shared = {"vecs": vecs, "cmask": cm, "wr": lay_big(inp["rw_r"][0]), "wk": lay_big(inp["rw_k"][0]), "wv": lay_big(inp["rw_v"][0]),
              "wla": lay_a(inp["rw_w_la"][0]), "ala": lay_a(inp["rw_a_la"][0]), "gla": lay_a(inp["rw_g_la"][0]),
              "wlb": np.ascontiguousarray(inp["rw_w_lb"][0], np.float32), "alb": np.ascontiguousarray(inp["rw_a_lb"][0], np.float32),
              "glb": np.ascontiguousarray(np.asarray(inp["rw_g_lb"][0], np.float32).reshape(2, 128, D).transpose(1, 0, 2))}
    maps = []
    for (b, s0) in core_slices(B, Sq):
        ht = np.zeros((D, NT + 1), np.float32)
        if s0 > 0:
            ht[:, 0] = h1[b, s0 - 1]
        ht[:, 1:] = h1[b, s0:s0 + NT].T
        m = dict(shared)
        m["hT"] = ht
        maps.append(m)
    res = run_bass_kernel_spmd(nc, maps, core_ids=list(range(8)))
    out = {}
    for n in ("At", "Rt", "Bh", "Kh", "gg", "bonus"):
        a = np.zeros((B, Sq, D), np.float32)
        for i, (b, s0) in enumerate(core_slices(B, Sq)):
            a[b, s0:s0 + NT] = res.results[i][n].T
        out[n] = a
    a = np.zeros((B, Sq, D), np.float32)
    gc = np.zeros((B, Sq // CH, D), np.float32)
    for i, (b, s0) in enumerate(core_slices(B, Sq)):
        a[b, s0:s0 + NT] = res.results[i]["raw"][2].T
        gc[b, s0 // CH:s0 // CH + NCHK] = res.results[i]["gC"].T
    out["v"] = a
    out["gC"] = gc
    return out


NSLOT = 4


def build_b(nch=64, group=2):
    C = Ctx()
    S = C.S
    fm_d = C.dram_in("fm", [nch, 128, 4, 4, CH])
    tm_d = C.dram_in("tm", [nch, 128, 4, 5, CH])
    gc_d = C.dram_in("gc", [nch, 128, 4, CH])
    cst_d = C.dram_in("cst", [128, 4, 256])
    y_d = C.dram_out("y", [nch, 128, 4, CH])

    cst = C.sb([128, 4, 256], BF16)
    dcst = Dep()
    S.dmak("pool", [], [dcst], out=cst[:], in_=cst_d)
    m13 = cst[:, :, 0:128]
    msl = cst[:, :, 128:192]
    id4 = cst[:, :, 192:256]
    ident = cst[:, 0, 192:256]

    def tiles(shape, dt, n=NSLOT):
        t = C.sb([128, n] + list(shape), dt)
        return t, [Dep() for _ in range(n)]

    FM, dFM = tiles([4, 4, CH], BF16)
    TM, dTM = tiles([4, 5, CH], BF16)
    GC, dGC = tiles([4, CH], F32)
    A13, dA13 = tiles([4, 128], BF16)
    A24, dA24 = tiles([4, 128], BF16)
    LL, dLL = tiles([2, 4, 2, CH], BF16)
    PP, dPP = tiles([2, 4, CH], BF16)
    AU, dAU = tiles([4, 128], BF16)
    XG, dXG = tiles([4, 128], BF16)
    QQ, dQQ = tiles([4, CH], BF16)
    YT, dYT = tiles([4, CH], F32)
    ST = C.sb([128, 2, 4, CH], BF16)
    dST = [Dep(), Dep()]
    S.ins("pool", "memset", [], [dST[0]], ap=ST[:, 0], constant=0.0)
    dLLh = [[Dep(), Dep()] for _ in range(NSLOT)]
    dPPh = [[Dep(), Dep()] for _ in range(NSLOT)]

    def mm_group(out_fn, specs, reads, dps):
        n = len(specs) * 8
        i = 0
        for p in range(4):
            for j in range(2):
                P = slice(64 * j, 64 * j + 64)
                for (ocs, lf, rf, st, sp) in specs:
                    i += 1
                    S.ins("pe", "matmul", reads, [dps], inc=(i == n), out=out_fn(P, p, ocs), lhsT=lf(P, p), rhs=rf(P, p),
                          start=st, stop=sp, tile_position=(64 * j, 64 * j))

    def v3(ps, w):
        return ps[:, 0:4 * w].rearrange("q (p w) -> q p w", p=4)

    def load(c):
        s = c % NSLOT
        S.dmak("pool", [], [dFM[s]], out=FM[:, s], in_=fm_d[c])
        S.dmak("pool", [], [dTM[s]], out=TM[:, s], in_=tm_d[c])
        S.dmak("sp", [], [dGC[s]], out=GC[:, s], in_=gc_d[c])

    def st_gram(c):
        s = c % NSLOT
        ps1, d1 = S.bank()
        ps2, d2 = S.bank()
        ps3, d3 = S.bank()
        o128 = lambda ps: (lambda P, p, ocs: v3(ps, 128)[P, p, :])
        o64 = lambda ps: (lambda P, p, ocs: v3(ps, 64)[P, p, :])
        AR = lambda P, p: FM[P, s, p, 0:2, :].rearrange("q a t -> q (a t)")
        mm_group(o128(ps1), [(None, lambda P, p: FM[P, s, p, 2, :], AR, True, True)], [dFM[s]], d1)
        mm_group(o128(ps2), [(None, lambda P, p: FM[P, s, p, 3, :], AR, True, True)], [dFM[s]], d2)
        mm_group(o64(ps3), [(None, lambda P, p: FM[P, s, p, 0, :], lambda P, p: FM[P, s, p, 2, :], True, True)], [dFM[s]], d3)
        S.ins("dve", "tensor_tensor", [d1, dcst], [dA13[s]], out=A13[:, s], in0=v3(ps1, 128), in1=m13, op=ALU.mult)
        S.ins("dve", "tensor_tensor", [d2, dcst], [dA24[s]], out=A24[:, s], in0=v3(ps2, 128), in1=m13, op=ALU.mult)
        S.ins("dve", "tensor_tensor", [d3, dcst], [dLLh[s][0]], out=LL[:, s, 0, :, 1, :], in0=v3(ps3, 64), in1=msl, op=ALU.mult)
        S.ins("act", "activation", [dA13[s]], [dLLh[s][0]], out=LL[:, s, 0, :, 0, :], in_=A13[:, s, :, 0:64], func=AF.Copy)
        S.ins("dve", "tensor_tensor", [dA13[s], dcst], [dPPh[s][0]], out=PP[:, s, 0], in0=A13[:, s, :, 0:64], in1=id4, op=ALU.add)

    def st_level(lev):
        def f(c):
            s = c % NSLOT
            a = (lev - 1) % 2
            b = lev % 2
            Lo = lambda P, p: LL[P, s, a, p, 0, :]
            LTo = lambda P, p: LL[P, s, a, p, 1, :]
            ps, dps = S.bank()
            pv = ps[:, 0:512].rearrange("q (p x w) -> q p x w", p=4, x=2)
            specs = [(None, Lo, LTo, True, True)]
            mm_group(lambda P, p, ocs: pv[P, p, 1, :], specs, [dLLh[s][a]], dps)
            if lev < 5:
                mm_group(lambda P, p, ocs: pv[P, p, 0, :], [(None, LTo, Lo, True, True)], [dLLh[s][a]], dps)
                S.ins("act", "activation", [dps], [dLLh[s][b]], out=LL[:, s, b], in_=pv, func=AF.Copy)
            else:
                S.ins("act", "activation", [dps], [dLLh[s][b]], out=LL[:, s, b, :, 1, :], in_=pv[:, :, 1, :], func=AF.Copy)
            ps2, dp2 = S.bank()
            mm_group(lambda P, p, ocs: v3(ps2, 64)[P, p, :], [(None, lambda P, p: LL[P, s, b, p, 1, :], lambda P, p: PP[P, s, a, p, :], True, True)],
                     [dLLh[s][b], dPPh[s][a]], dp2)
            S.ins("dve", "tensor_tensor", [dp2, dPPh[s][a]], [dPPh[s][b]], out=PP[:, s, b], in0=v3(ps2, 64), in1=PP[:, s, a], op=ALU.add)
        return f

    TI = 5 % 2

    def st_wl(c):
        s = c % NSLOT
        ps, dps = S.bank()
        mm_group(lambda P, p, ocs: v3(ps, 64)[P, p, :], [(None, lambda P, p: A24[P, s, p, 0:64], lambda P, p: TM[P, s, p, 4, :], True, True)],
                 [dA24[s], dTM[s]], dps)
        S.ins("act", "activation", [dps], [dTM[s]], out=TM[:, s, :, 1, :], in_=v3(ps, 64), func=AF.Copy)

    def st_au(c):
        s = c % NSLOT
        ps, dps = S.bank()
        mm_group(lambda P, p, ocs: v3(ps, 128)[P, p, :],
                 [(None, lambda P, p: PP[P, s, TI, p, :], lambda P, p: TM[P, s, p, 0:2, :].rearrange("q a t -> q (a t)"), True, True)],
                 [dPPh[s][TI], dTM[s]], dps)
        S.ins("act", "activation", [dps], [dAU[s]], out=AU[:, s], in_=v3(ps, 128), func=AF.Copy)

    def st_xg(c):
        s = c % NSLOT
        ps, dps = S.bank()
        pv = v3(ps, 128)
        idl = lambda P, p: ident[P, :]
        mm_group(lambda P, p, ocs: pv[P, p, ocs], [
            (slice(0, 64), lambda P, p: AU[P, s, p, 0:64], lambda P, p: TM[P, s, p, 2, :], True, False),
            (slice(0, 64), idl, idl, False, True),
            (slice(64, 128), lambda P, p: TM[P, s, p, 2, :], lambda P, p: AU[P, s, p, 64:128], True, False),
            (slice(64, 128), lambda P, p: TM[P, s, p, 3, :], lambda P, p: TM[P, s, p, 4, :], False, True),
        ], [dAU[s], dTM[s], dcst], dps)
        S.ins("act", "activation", [dps], [dXG[s]], out=XG[:, s], in_=pv, func=AF.Copy)

    def st_q(c):
        s = c % NSLOT
        ps, dps = S.bank()
        idl = lambda P, p: ident[P, :]
        mm_group(lambda P, p, ocs: v3(ps, 64)[P, p, :], [
            (None, lambda P, p: AU[P, s, p, 0:64], lambda P, p: A13[P, s, p, 64:128], True, False),
            (None, idl, lambda P, p: FM[P, s, p, 1, :], False, True),
        ], [dAU[s], dA13[s], dFM[s], dcst], dps)
        S.ins("act", "activation", [dps], [dQQ[s]], out=QQ[:, s], in_=v3(ps, 64), func=AF.Copy)

    douts = []

    def st_state(c):
        s = c % NSLOT
        cur = c % 2
        nxt = (c + 1) % 2
        idl = lambda P, p: ident[P, :]
        psy, dy = S.bank()
        mm_group(lambda P, p, ocs: v3(psy, 64)[P, p, :], [
            (None, lambda P, p: A13[P, s, p, 64:128], lambda P, p: AU[P, s, p, 64:128], True, False),
            (None, lambda P, p: A24[P, s, p, 64:128], lambda P, p: TM[P, s, p, 4, :], False, False),
            (None, lambda P, p: QQ[P, s, p, :], lambda P, p: ST[P, cur, p, :], False, True),
        ], [dA13[s], dA24[s], dAU[s], dTM[s], dQQ[s], dST[cur]], dy)
        pss, ds = S.bank()
        mm_group(lambda P, p, ocs: v3(pss, 64)[P, p, :], [
            (None, lambda P, p: XG[P, s, p, 0:64], lambda P, p: ST[P, cur, p, :], True, False),
            (None, idl, lambda P, p: XG[P, s, p, 64:128], False, True),
        ], [dXG[s], dST[cur], dcst], ds)
        S.ins("dve", "tensor_tensor", [ds, dGC[s]], [dST[nxt]], out=ST[:, nxt], in0=v3(pss, 64), in1=GC[:, s], op=ALU.mult)
        S.ins("act", "activation", [dy], [dYT[s]], out=YT[:, s], in_=v3(psy, 64), func=AF.Copy)
        S.dmak("sp", [dYT[s]], [], owner=dYT[s], out=y_d[c], in_=YT[:, s])

    local_stages = [st_gram] + [st_level(l) for l in range(1, 6)] + [st_wl, st_au, st_xg, st_q]
    for g0 in range(0, nch, group):
        cs = list(range(g0, min(nch, g0 + group)))
        for c in cs:
            load(c)
        for stg_ in local_stages:
            for c in cs:
                stg_(c)
        for c in cs:
            st_state(c)
    S.wait("sp", dYT)
    return C.finish()


def b_consts():
    s = np.arange(64)
    su = (s[:, None] < s[None, :]).astype(np.float32)
    iu = (s[:, None] <= s[None, :]).astype(np.float32)
    sl = (s[:, None] > s[None, :]).astype(np.float32)
    idn = np.eye(64, dtype=np.float32)
    one = np.concatenate([su, iu, sl, idn], 1)
    one = np.concatenate([one, one], 0)
    return np.ascontiguousarray(np.broadcast_to(one[:, None, :], (128, 4, 256)))


def b_layout(At, Rt, Bh, Kh, V, gC, b, hg, nch=64):
    cs = slice(hg * 512, (hg + 1) * 512)

    def fm(x):
        a = x[b, :nch * CH, cs].reshape(nch, CH, 4, 128)
        return a.transpose(0, 3, 2, 1)

    def tm(x):
        a = x[b, :nch * CH, cs].reshape(nch, CH, 4, 2, 64)
        return a.transpose(0, 3, 1, 2, 4).reshape(nch, 128, 4, 64)

    FMa = np.stack([fm(At), fm(Rt), fm(Bh), fm(Kh)], 3)
    z = np.zeros((nch, 128, 4, 64), np.float32)
    TMa = np.stack([tm(At), z, tm(Bh), tm(Kh), tm(V)], 3)
    g = gC[b, :nch, cs].reshape(nch, 4, 128).transpose(0, 2, 1)
    GCa = np.broadcast_to(g[:, :, :, None], (nch, 128, 4, CH))
    return {"fm": np.ascontiguousarray(FMa, np.float32), "tm": np.ascontiguousarray(TMa, np.float32),
            "gc": np.ascontiguousarray(GCa, np.float32), "cst": b_consts()}


def run_b(pre, nch=64):
    B = pre["At"].shape[0]
    nc = build_b(nch)
    maps = []
    for b in range(B):
        for hg in range(4):
            maps.append(b_layout(pre["At"], pre["Rt"], pre["Bh"], pre["Kh"], pre["v"], pre["gC"], b, hg, nch))
    res = run_bass_kernel_spmd(nc, maps, core_ids=list(range(8)))
    y = np.zeros((B, nch * CH, D), np.float32)
    for i in range(8):
        b, hg = divmod(i, 4)
        a = res.results[i]["y"].reshape(nch, 2, CH, 4, 64)
        y[b, :, hg * 512:(hg + 1) * 512] = a.transpose(0, 2, 3, 1, 4).reshape(nch * CH, 512)
    return y


GN_EPS = 64e-5


def build_c():
    C = Ctx()
    S = C.S
    hT = C.dram_in("hT", [D, NT])
    yT = C.dram_in("yT", [D, NT])
    bT = C.dram_in("bT", [D, NT])
    gT_d = C.dram_in("gT", [D, NT])
    vecs = C.dram_in("vecs", [128, 4 * NC16])
    wo_d = C.dram_in("wo", [NC16, 128, NC16, 128])
    w13_d = C.dram_in("w13", [NFC, 128, 2, NC16, 128])
    w2_d = C.dram_in("w2", [NSPLIT, NC16, 128, FPS, 128])
    oT = C.dram_out("oT", [D, NT])
    v_ = lambda a: a.rearrange("(c p) t -> p c t", p=128)
    hv, yv, bv, gv, ov = v_(hT), v_(yT), v_(bT), v_(gT_d), v_(oT)

    load_consts(C)
    make_eps(C)
    vt, dv = load_vec(C, vecs, 4 * NC16)
    gne = C.sb([128, 1], F32)
    dgne = Dep()
    S.ins("pool", "memset", [], [dgne], ap=gne[:], constant=GN_EPS)
    blk = C.sb([128, 128], BF16)
    bq = C.sb([128, 2, NT], BF16)
    dbq = [Dep(), Dep()]
    dblk = Dep()
    S.ins("pool", "memset", [], [dblk], ap=blk[:], constant=0.0)
    S.ins("pool", "memset", [], [dblk], ap=blk[0:64, 0:64], constant=1.0)
    S.ins("pool", "memset", [], [dblk], ap=blk[64:128, 64:128], constant=1.0)

    h = C.sb([128, NC16, NT], F32)
    dh = [Dep() for _ in range(NC16)]
    for c in range(NC16):
        S.dmak("sp", [], [dh[c]], out=h[:, c, :], in_=hv[:, c, :])
    z = C.sb([128, NC16, NT], BF16)
    dz = [Dep() for _ in range(NC16)]
    scr = C.sb([128, 17408], F32)
    NB = 2
    sin = scr[:, 0:NB * 3 * NT].rearrange("p (b i t) -> p b i t", b=NB, i=3)
    dsin = [[Dep() for _ in range(3)] for _ in range(NB)]
    tmp = scr[:, NB * 3 * NT:NB * 3 * NT + NB * 2 * NT].rearrange("p (b i t) -> p b i t", b=NB, i=2)
    dtm = [[Dep() for _ in range(2)] for _ in range(NB)]

    for oc in range(NC16):
        b = oc % NB
        for i, src in enumerate((yv, bv, gv)):
            S.dmak("sp", [], [dsin[b][i]], out=sin[:, b, i, :], in_=src[:, oc, :])
        Y, BO, G = range(3)
        d = dsin[b]
        t = dtm[b]
        for h2 in range(2):
            cs = slice(h2 * 512, (h2 + 1) * 512)
            ps, dps = S.bank()
            S.ins("act", "activation", [d[Y]], [dbq[b]], out=bq[:, b, cs], in_=sin[:, b, Y, cs], func=AF.Copy)
            S.ins("pe", "matmul", [dblk, dbq[b]], [dps], out=ps[:], lhsT=blk[:], rhs=bq[:, b, cs], start=True, stop=True)
            S.ins("dve", "scalar_tensor_tensor", [dps, d[Y]], [t[0]], out=tmp[:, b, 0, cs], in0=ps[:], scalar=-1.0 / 64, in1=sin[:, b, Y, cs],
                  op0=ALU.mult, op1=ALU.add)
            S.ins("act", "activation", [t[0]], [dbq[b]], out=bq[:, b, cs], in_=tmp[:, b, 0, cs], func=AF.Square)
            ps2, dp2 = S.bank()
            S.ins("pe", "matmul", [dblk, dbq[b]], [dp2], out=ps2[:], lhsT=blk[:], rhs=bq[:, b, cs], start=True, stop=True)
            S.ins("act", "activation", [dp2, dgne], [t[1]], out=tmp[:, b, 1, cs], in_=ps2[:], func=AF.Sqrt, bias=gne[:, 0:1], scale=1.0 / 64)
        S.ins("dve", "reciprocal", [t[1]], [t[1]], out=tmp[:, b, 1, :], in_=tmp[:, b, 1, :])
        S.ins("dve", "tensor_tensor", [t[0], t[1]], [t[0]], out=tmp[:, b, 0, :], in0=tmp[:, b, 0, :], in1=tmp[:, b, 1, :], op=ALU.mult)
        S.ins("act", "activation", [t[0], dv], [t[0]], out=tmp[:, b, 0, :], in_=tmp[:, b, 0, :], func=AF.Identity,
              bias=vt[:, NC16 + oc:NC16 + oc + 1], scale=vt[:, oc:oc + 1])
        S.ins("dve", "tensor_tensor", [t[0], d[BO]], [t[0]], out=tmp[:, b, 0, :], in0=tmp[:, b, 0, :], in1=sin[:, b, BO, :], op=ALU.add)
        S.ins("dve", "tensor_tensor", [t[0], d[G]], [dz[oc]], out=z[:, oc, :], in0=tmp[:, b, 0, :], in1=sin[:, b, G, :], op=ALU.mult)

    wt = C.sb([128, 3, NC16, 128], BF16)
    dwt = [Dep() for _ in range(3)]
    for oc in range(NC16):
        wb = oc % 3
        S.dmak("pool", [], [dwt[wb]], out=wt[:, wb], in_=wo_d[oc])
        bk = [S.bank() for _ in range(2)]
        for kc in range(NC16):
            for tt in range(2):
                ps, dps = bk[tt]
                S.ins("pe", "matmul", [dwt[wb], dz[kc]], [dps], inc=(kc == NC16 - 1), out=ps[:], lhsT=wt[:, wb, kc, :],
                      rhs=z[:, kc, tt * 512:(tt + 1) * 512], start=(kc == 0), stop=(kc == NC16 - 1))
        for tt in range(2):
            ps, dps = bk[tt]
            cs = slice(tt * 512, (tt + 1) * 512)
            S.ins("dve", "tensor_tensor", [dps], [dh[oc]], out=h[:, oc, cs], in0=h[:, oc, cs], in1=ps[:], op=ALU.add)

    rstd = C.sb([128, NT], F32)
    d_rstd = Dep()
    sq = C.sb([128, 2, 512], BF16)
    d_sq = [Dep(), Dep()]
    rms_rstd(C, lambda c, o, n: h[:, c, o:o + n], dh, [(0, 512), (512, 512)], rstd, d_rstd, sq, d_sq)
    norm_apply(C, h, dh, 0, NT, vt, dv, 2 * NC16, rstd, d_rstd, z, dz)
    gTt = scr.bitcast(BF16)[:, 0:2 * FPS * NT].rearrange("p (b f t) -> p b f t", b=2, f=FPS)
    dgT = [[Dep() for _ in range(FPS)] for _ in range(2)]
    o13 = 2 * FPS * NT
    w13t = scr.bitcast(BF16)[:, o13:o13 + 3 * 2 * NC16 * 128].rearrange("p (b m k f) -> p b m k f", b=3, m=2, k=NC16)
    dw13 = [Dep() for _ in range(3)]
    alias_barrier([x for l in dsin for x in l] + [x for l in dtm for x in l], [x for l in dgT for x in l] + dw13)
    w2t = C.sb([128, 3, FPS, 128], BF16)
    dw2 = [Dep() for _ in range(3)]
    ftmp = C.sb([128, 2, 512], F32)
    dftmp = [Dep(), Dep()]
    ffn_block(C, h, dh, 0, z, dz, w13_d, w2_d, gTt, dgT, w13t, dw13, w2t, dw2, ftmp, dftmp)

    rms_rstd(C, lambda c, o, n: h[:, c, o:o + n], dh, [(0, 512), (512, 512)], rstd, d_rstd, sq, d_sq)
    for c in range(NC16):
        S.ins("dve", "scalar_tensor_tensor", [dh[c], dv, d_rstd], [dh[c]], out=h[:, c, :], in0=h[:, c, :],
              scalar=vt[:, 3 * NC16 + c:3 * NC16 + c + 1], in1=rstd[:], op0=ALU.mult, op1=ALU.mult)
        S.dmak("sp", [dh[c]], [], owner=dh[c], out=ov[:, c, :], in_=h[:, c, :])
    S.wait("sp", dh)
    return C.finish()


def run_c(inp, h1, y, bonus, gg):
    B, Sq, _ = h1.shape
    nc = build_c()
    vecs = np.concatenate([pvec(inp["rw_lnx_w"][0]), pvec(inp["rw_lnx_b"][0]), pvec(inp["norm2_g"][1]), pvec(inp["final_g"])], 1)
    shared = {"vecs": vecs, "wo": lay_big(inp["rw_o"][0]),
              "w13": lay_w13(np.asarray(inp["ffn_w1"][1]), np.asarray(inp["ffn_w3"][1])), "w2": lay_w2(np.asarray(inp["ffn_w2"][1]))}
    maps = []
    T_ = lambda a, b, s0: np.ascontiguousarray(a[b, s0:s0 + NT].T)
    for (b, s0) in core_slices(B, Sq):
        m = dict(shared)
        m.update(hT=T_(h1, b, s0), yT=T_(y, b, s0), bT=T_(bonus, b, s0), gT=T_(gg, b, s0))
        maps.append(m)
    res = run_bass_kernel_spmd(nc, maps, core_ids=list(range(8)))
    out = np.zeros((B, Sq, D), np.float32)
    for i, (b, s0) in enumerate(core_slices(B, Sq)):
        out[b, s0:s0 + NT] = res.results[i]["oT"].T
    return out


def kernel(**inp):
    inp = {k: np.asarray(v) for k, v in inp.items()}
    h1 = run_a1(inp)
    pre = run_a2(inp, h1)
    y = run_b(pre)
    out = run_c(inp, h1, y, pre["bonus"], pre["gg"])
    return out.astype(np.float32)
```
